# Optimizing a Trainium2 kernel written in Bass

```python
import jax, jax.numpy as jnp
from jax import lax

D_MODEL = 1024
BATCH = 2
SEQ = 8192
DEPTH = 2

D_MIX = D_MODEL
D_RWKV = D_MIX // 2
D_CONV = D_MIX - D_RWKV
HEAD_DIM = 64
N_RWKV_HEADS = D_RWKV // HEAD_DIM
N_CONV_GROUPS = D_CONV // HEAD_DIM
W_LORA = 32
A_LORA = 32
V_LORA = 32
G_LORA = 96
CONV_WIDTH = 31
FFN_CONV_WIDTH = 3
D_FF = 2816
GLU_COLS = 2 * D_CONV
RWKV_COLS = 3 * D_RWKV + W_LORA + A_LORA + G_LORA
COLS_FIRST = GLU_COLS + RWKV_COLS
COLS_REST = COLS_FIRST + V_LORA
RMS_EPS = 1e-6
LN_EPS = 1e-5
GN_EPS = 64e-5

kernel_name = "rwkv7_conformer_hymba_convffn_trunk"


def _rms_norm(x, g):
    xf = x.astype(jnp.float32)
    y = xf * lax.rsqrt(jnp.mean(xf * xf, axis=-1, keepdims=True) + RMS_EPS)
    return (y * g.astype(jnp.float32)).astype(x.dtype)


def _layer_norm(x, g, b):
    xf = x.astype(jnp.float32)
    mu = jnp.mean(xf, axis=-1, keepdims=True)
    var = jnp.mean(jnp.square(xf - mu), axis=-1, keepdims=True)
    y = (xf - mu) * lax.rsqrt(var + LN_EPS)
    return (y * g.astype(jnp.float32) + b.astype(jnp.float32)).astype(x.dtype)


def _token_shift(p):
    return jnp.pad(p, ((0, 0), (1, 0), (0, 0)))[:, :-1]


def _causal_dwconv(x, w, b):
    K = w.shape[0]
    y = lax.conv_general_dilated(
        x, w[:, None, :].astype(x.dtype), window_strides=(1,), padding=((K - 1, 0),),
        dimension_numbers=("NWC", "WIO", "NWC"), feature_group_count=x.shape[-1])
    return y + b.astype(x.dtype)


def _wkv7(r, decay, k, v, a_vec, b_vec):
    B, T, H, N = r.shape

    def step(S, inp):
        r_t, w_t, k_t, v_t, a_t, b_t = inp
        sa = jnp.einsum("bhvk,bhk->bhv", S, a_t)
        S = S * w_t[:, :, None, :] + sa[..., None] * b_t[:, :, None, :] + v_t[..., None] * k_t[:, :, None, :]
        y = jnp.einsum("bhvk,bhk->bhv", S, r_t)
        return S, y

    xs = tuple(jnp.moveaxis(t.astype(jnp.float32), 1, 0) for t in (r, decay, k, v, a_vec, b_vec))
    S0 = jnp.zeros((B, H, N, N), jnp.float32)
    _, ys = lax.scan(step, S0, xs)
    return jnp.moveaxis(ys, 0, 1)


def _head_group_norm(y, g, b):
    B, T, H, N = y.shape
    mu = jnp.mean(y, axis=-1, keepdims=True)
    var = jnp.mean(jnp.square(y - mu), axis=-1, keepdims=True)
    y = ((y - mu) * lax.rsqrt(var + GN_EPS)).reshape(B, T, H * N)
    return y * g.astype(jnp.float32) + b.astype(jnp.float32)


def _hybrid_mixer(h, w_in, mu, vres_v0, vres_up, v_first, decay_w0, decay_up, iclr_a0, iclr_up,
                  gate_up, k_k, k_a, r_k, lnx_g, lnx_b, cconv_w, cconv_b, cln_g, cln_b, w_out):
    B, T, _ = h.shape
    H, N = N_RWKV_HEADS, HEAD_DIM
    p = h @ w_in.astype(h.dtype)
    u = p[..., :GLU_COLS]
    q = p[..., GLU_COLS:]
    q = q + (_token_shift(q) - q) * mu.astype(h.dtype)
    o1, o2, o3 = D_RWKV, 2 * D_RWKV, 3 * D_RWKV
    o4, o5, o6 = o3 + W_LORA, o3 + W_LORA + A_LORA, RWKV_COLS
    r, k, v = q[..., :o1], q[..., o1:o2], q[..., o2:o3]
    wd, ad, gd = q[..., o3:o4], q[..., o4:o5], q[..., o5:o6]

    wf = (decay_w0 + jnp.tanh(wd) @ decay_up).astype(jnp.float32)
    decay = jnp.exp(-jnp.exp(-jax.nn.softplus(-wf) - 0.5))
    if v_first is None:
        v_first = v
    else:
        vd = q[..., o6:]
        v = v + (v_first - v) * jax.nn.sigmoid(vres_v0 + vd @ vres_up)
    a = jax.nn.sigmoid(iclr_a0 + ad @ iclr_up)
    g = jax.nn.sigmoid(gd) @ gate_up
    kk = (k * k_k).reshape(B, T, H, N).astype(jnp.float32)
    kk = kk * lax.rsqrt(jnp.maximum(jnp.sum(kk * kk, axis=-1, keepdims=True), 1e-24))
    k = k * (1 + (a - 1) * k_a)
    r_h = r.reshape(B, T, H, N)
    k_h = k.reshape(B, T, H, N)
    v_h = v.reshape(B, T, H, N)
    a_h = a.reshape(B, T, H, N).astype(jnp.float32)
    y = _wkv7(r_h, decay.reshape(B, T, H, N), k_h, v_h, -kk, kk * a_h)
    y = _head_group_norm(y, lnx_g, lnx_b).astype(h.dtype)
    bonus = jnp.sum(r_h * k_h * r_k.astype(h.dtype), axis=-1, keepdims=True) * v_h
    rwkv_out = (y + bonus.reshape(B, T, D_RWKV)) * g

    c = u[..., :D_CONV] * jax.nn.sigmoid(u[..., D_CONV:])
    c = _causal_dwconv(c, cconv_w, cconv_b)
    c = jax.nn.silu(_layer_norm(c, cln_g, cln_b))

    mix = jnp.concatenate([rwkv_out, c], axis=-1) @ w_out.astype(h.dtype)
    return mix, v_first


def _conv_ffn(h, ffn_up, ffn_conv_w, ffn_conv_b, ffn_down):
    hid = h @ ffn_up.astype(h.dtype)
    hid = _causal_dwconv(hid, ffn_conv_w, ffn_conv_b)
    gate, up = hid[..., :D_FF], hid[..., D_FF:]
    return (jax.nn.gelu(gate, approximate=True) * up) @ ffn_down.astype(h.dtype)


def setup_inputs(seed: int = 0) -> dict:
    key = jax.random.key(seed)
    ks = iter(jax.random.split(key, 40))
    f32 = jnp.float32

    def nrm(shape, scale):
        return jax.random.normal(next(ks), shape, f32) * scale

    def unif(shape, lo, hi):
        return jax.random.uniform(next(ks), shape, f32, lo, hi)

    L, Lr = DEPTH, DEPTH - 1
    return {
        "x": nrm((BATCH, SEQ, D_MODEL), 1.0),
        "w_in_first": nrm((D_MODEL, COLS_FIRST), D_MODEL ** -0.5),
        "mu_first": unif((RWKV_COLS,), 0.0, 1.0),
        "w_in_rest": nrm((Lr, D_MODEL, COLS_REST), D_MODEL ** -0.5),
        "mu_rest": unif((Lr, RWKV_COLS + V_LORA), 0.0, 1.0),
        "vres_v0": nrm((Lr, D_RWKV), 0.5),
        "vres_up": nrm((Lr, V_LORA, D_RWKV), 0.1),
        "decay_w0": unif((L, D_RWKV), -6.0, 1.0),
        "decay_up": nrm((L, W_LORA, D_RWKV), 0.1),
        "iclr_a0": nrm((L, D_RWKV), 0.5),
        "iclr_up": nrm((L, A_LORA, D_RWKV), 0.1),
        "gate_up": nrm((L, G_LORA, D_RWKV), G_LORA ** -0.5),
        "k_k": 0.85 + nrm((L, D_RWKV), 0.05),
        "k_a": 1.0 + nrm((L, D_RWKV), 0.05),
        "r_k": nrm((L, N_RWKV_HEADS, HEAD_DIM), 0.1),
        "lnx_g": 1.0 + nrm((L, D_RWKV), 0.05),
        "lnx_b": nrm((L, D_RWKV), 0.01),
        "cconv_w": nrm((L, CONV_WIDTH, D_CONV), CONV_WIDTH ** -0.5),
        "cconv_b": nrm((L, D_CONV), 0.01),
        "cln_g": 1.0 + nrm((L, D_CONV), 0.05),
        "cln_b": nrm((L, D_CONV), 0.01),
        "w_out": nrm((L, D_MIX, D_MODEL), D_MIX ** -0.5),
        "norm_pre_mix": 1.0 + nrm((L, D_MODEL), 0.05),
        "norm_post_mix": 1.0 + nrm((L, D_MODEL), 0.05),
        "norm_pre_ffn": 1.0 + nrm((L, D_MODEL), 0.05),
        "norm_post_ffn": 1.0 + nrm((L, D_MODEL), 0.05),
        "ffn_up": nrm((L, D_MODEL, 2 * D_FF), D_MODEL ** -0.5),
        "ffn_conv_w": nrm((L, FFN_CONV_WIDTH, 2 * D_FF), FFN_CONV_WIDTH ** -0.5),
        "ffn_conv_b": nrm((L, 2 * D_FF), 0.01),
        "ffn_down": nrm((L, D_FF, D_MODEL), D_FF ** -0.5),
    }


def reference(x, w_in_first, mu_first, w_in_rest, mu_rest, vres_v0, vres_up, decay_w0, decay_up,
              iclr_a0, iclr_up, gate_up, k_k, k_a, r_k, lnx_g, lnx_b, cconv_w, cconv_b, cln_g, cln_b,
              w_out, norm_pre_mix, norm_post_mix, norm_pre_ffn, norm_post_ffn, ffn_up, ffn_conv_w,
              ffn_conv_b, ffn_down):
    v_first = None
    for i in range(DEPTH):
        if i == 0:
            w_in_i, mu_i, v0_i, vup_i = w_in_first, mu_first, None, None
        else:
            w_in_i, mu_i, v0_i, vup_i = w_in_rest[i - 1], mu_rest[i - 1], vres_v0[i - 1], vres_up[i - 1]
        h = _rms_norm(x, norm_pre_mix[i])
        mix, v_first = _hybrid_mixer(
            h, w_in_i, mu_i, v0_i, vup_i, v_first, decay_w0[i], decay_up[i], iclr_a0[i], iclr_up[i],
            gate_up[i], k_k[i], k_a[i], r_k[i], lnx_g[i], lnx_b[i], cconv_w[i], cconv_b[i],
            cln_g[i], cln_b[i], w_out[i])
        x = x + _rms_norm(mix, norm_post_mix[i])
        h = _rms_norm(x, norm_pre_ffn[i])
        f = _conv_ffn(h, ffn_up[i], ffn_conv_w[i], ffn_conv_b[i], ffn_down[i])
        x = x + _rms_norm(f, norm_post_ffn[i])
    return x
```

```python
import numpy as np
from contextlib import ExitStack
import concourse.bass as bass
import concourse.mybir as mybir
from concourse.bass_utils import run_bass_kernel_spmd

F32 = mybir.dt.float32
BF16 = mybir.dt.bfloat16
ALU = mybir.AluOpType
AF = mybir.ActivationFunctionType

NCORES = 8
D = 1024
SEQ = 8192
NT = 2048
HALO = 32
NP = 1024
NPASS = NT // NP
PW = HALO + NP
DR = 512
DFF = 2816
NFF = DFF // 128
CW = 31
ENGS = ("tensor", "vector", "scalar", "gpsimd", "sync")
DBG = {"stop": None}


class _Stop(Exception):
    pass


def _chk(tag):
    if DBG["stop"] == tag:
        raise _Stop()


class Buf:
    def __init__(self, name, t, parent=None):
        self.name = name
        self.t = t
        self.lw = None
        self.rd = {}
        self.parent = parent
        self.excl = False

    def root(self):
        return self.parent.root() if self.parent is not None else self

    def __getitem__(self, idx):
        return self.t[idx]


class Sched:
    def __init__(self, nc, es):
        self.nc = nc
        self.es = es
        self.q = {e: [] for e in ENGS}
        self.sem = {}
        self.cnt = {}
        self.seen = {e: {} for e in ENGS}
        for e in ("tensor", "vector", "scalar", "gpsimd"):
            self.sem[e] = es.enter_context(nc.semaphore("s_" + e))
            self.cnt[e] = 0
        self.nbuf = 0
        self.store_keys = []

    def sb(self, name, shape, dt=F32):
        self.nbuf += 1
        t = self.es.enter_context(self.nc.sbuf_tensor(f"{name}_{self.nbuf}", list(shape), dt))
        return Buf(name, t)

    def ps(self, name, shape, dt=F32):
        self.nbuf += 1
        t = self.es.enter_context(self.nc.psum_tensor(f"{name}_{self.nbuf}", list(shape), dt))
        return Buf(name, t)

    def view(self, name, ap, parent=None):
        return Buf(name, ap, parent)

    def _dsem(self, key):
        if key not in self.sem:
            self.sem[key] = self.es.enter_context(self.nc.semaphore("d_" + key))
            self.cnt[key] = 0
        return self.sem[key]

    def _need(self, eng, deps):
        for key, val in deps:
            if self.seen[eng].get(key, 0) >= val:
                continue
            self.seen[eng][key] = val
            sem = self.sem[key]
            self.q[eng].append(lambda e, sem=sem, val=val: e.wait_ge(sem, val))

    @staticmethod
    def _deps(reads, writes, eng=None):
        deps = []
        for b in reads:
            if b.lw is not None:
                deps.append(b.lw)
            if b.excl:
                deps.extend((k, v) for (k, v) in b.rd.items() if k != eng)
        for b in writes:
            if b.lw is not None:
                deps.append(b.lw)
            deps.extend(b.rd.items())
        return deps

    def op(self, eng, fn, reads=(), writes=()):
        reads = [b.root() for b in reads]
        writes = [b.root() for b in writes]
        deps = self._deps(reads, writes, eng)
        if eng == "tensor":
            deps = [(k, v) for (k, v) in deps if k != eng]
        self._need(eng, deps)
        self.cnt[eng] += 1
        n = self.cnt[eng]
        sem = self.sem[eng]
        self.q[eng].append(lambda e, fn=fn, sem=sem: fn(e).then_inc(sem, 1))
        for b in writes:
            b.lw = (eng, n)
            b.rd = {}
        for b in reads:
            if b.rd.get(eng, 0) < n:
                b.rd[eng] = n

    def dma(self, eng, out_ap, in_ap, reads=(), writes=(), store=False):
        reads = [b.root() for b in reads]
        writes = [b.root() for b in writes]
        key = ("L%d" % id(writes[0])) if writes else ("S%d" % id(reads[0]))
        sem = self._dsem(key)
        self._need(eng, self._deps(reads, writes))
        self.cnt[key] += 16
        n = self.cnt[key]
        self.q[eng].append(lambda e, o=out_ap, i=in_ap, sem=sem: e.dma_start(out=o, in_=i).then_inc(sem, 16))
        for b in writes:
            b.lw = (key, n)
            b.rd = {}
        for b in reads:
            b.rd[key] = n
        if store and key not in self.store_keys:
            self.store_keys.append(key)

    def finish(self, eng="sync"):
        for key in self.store_keys:
            sem, val = self.sem[key], self.cnt[key]
            self.q[eng].append(lambda e, sem=sem, val=val: e.wait_ge(sem, val))
        for k in ("tensor", "vector", "scalar", "gpsimd"):
            if self.cnt[k] > 0:
                sem, val = self.sem[k], self.cnt[k]
                self.q[eng].append(lambda e, sem=sem, val=val: e.wait_ge(sem, val))

    def emit(self):
        with self.nc.Block() as block:
            for e in ENGS:
                if not self.q[e]:
                    continue

                def body(eng, ops=self.q[e]):
                    for f in ops:
                        f(eng)
                getattr(block, e)(body)


class Ops:
    def __init__(self, S):
        self.S = S

    def mm(self, out, lhsT, rhs, reads, writes, start=True, stop=True):
        self.S.op("tensor", lambda e: e.matmul(out, lhsT, rhs, start=start, stop=stop), reads, writes)

    def tr(self, out, in_, ident, reads, writes):
        self.S.op("tensor", lambda e: e.transpose(out, in_, ident), reads, writes)

    def act(self, out, in_, func, reads, writes, bias=None, scale=1.0):
        kw = {} if bias is None else {"bias": bias}
        self.S.op("scalar", lambda e: e.activation(out=out, in_=in_, func=func, scale=scale, **kw), reads, writes)

    def tt(self, eng, out, a, b, op, reads, writes):
        self.S.op(eng, lambda e: e.tensor_tensor(out=out, in0=a, in1=b, op=op), reads, writes)

    def ts(self, eng, out, a, s1, s2, op0, op1, reads, writes):
        if s2 is None:
            self.S.op(eng, lambda e: e.tensor_scalar(out=out, in0=a, scalar1=s1, scalar2=None, op0=op0), reads, writes)
        else:
            self.S.op(eng, lambda e: e.tensor_scalar(out=out, in0=a, scalar1=s1, scalar2=s2, op0=op0, op1=op1), reads, writes)

    def stt(self, out, a, sc, b, op0, op1, reads, writes):
        self.S.op("vector", lambda e: e.scalar_tensor_tensor(out=out, in0=a, scalar=sc, in1=b, op0=op0, op1=op1), reads, writes)

    def cp(self, eng, out, in_, reads, writes):
        if eng == "scalar":
            self.act(out, in_, AF.Copy, reads, writes)
        else:
            self.S.op(eng, lambda e: e.tensor_copy(out=out, in_=in_), reads, writes)

    def recip(self, out, in_, reads, writes):
        self.S.op("vector", lambda e: e.reciprocal(out=out, in_=in_), reads, writes)

    def scan(self, out, d0, d1, reads, writes):
        self.S.op("vector", lambda e: e.tensor_tensor_scan(out=out, data0=d0, data1=d1, initial=0.0, op0=ALU.mult, op1=ALU.add), reads, writes)

    def memset(self, eng, ap, val, writes):
        self.S.op(eng, lambda e: e.memset(ap, val), (), writes)


def _consts():
    c = {}
    ident = np.eye(128, dtype=np.float32)
    s = np.arange(64)
    iu = (s[:, None] < s[None, :]).astype(np.float32)
    iu0 = (s[:, None] <= s[None, :]).astype(np.float32)
    il = iu.T.copy()
    m_ab = np.tile(np.concatenate([iu, iu0], 1), (2, 1))
    m_nk = np.tile(np.concatenate([il, il], 1), (2, 1))
    m_kr = np.tile(iu0, (2, 1))
    blk = np.kron(np.eye(2, dtype=np.float32), np.ones((64, 64), np.float32))
    ones = np.ones((128, 128), np.float32)
    id64 = np.tile(np.eye(64, dtype=np.float32), (2, 1))
    rmask = np.ones((128, NP), np.float32)
    rmask[:, ::64] = 0.0
    c["cA"] = np.concatenate([ident, m_ab, m_nk, m_kr, blk, ones, id64], 1)
    c["rmask"] = rmask
    return c


CA_ID, CA_AB, CA_NK, CA_KR, CA_BLK, CA_ONES, CA_ID64 = 0, 128, 256, 384, 448, 576, 704
CA_W = 768


class VecPack:
    def __init__(self):
        self.cols = []
        self.idx = {}

    def add(self, name, v):
        v = np.asarray(v, np.float32).reshape(-1)
        col = np.zeros(128, np.float32)
        col[: v.shape[0]] = v
        self.idx[name] = len(self.cols)
        self.cols.append(col)

    def array(self):
        return np.stack(self.cols, 1).copy()


def _pack_A(layer, inp):
    vp = VecPack()
    g = inp["norm_pre_mix"][layer]
    for k in range(8):
        vp.add(f"g{k}", g[k * 128:(k + 1) * 128])
    mu = inp["mu_first"] if layer == 0 else inp["mu_rest"][layer - 1]
    for j in range(4):
        vp.add(f"mu_r{j}", mu[0 * DR + j * 128: 0 * DR + (j + 1) * 128])
        vp.add(f"mu_k{j}", mu[1 * DR + j * 128: 1 * DR + (j + 1) * 128])
        vp.add(f"mu_v{j}", mu[2 * DR + j * 128: 2 * DR + (j + 1) * 128])
    o = 3 * DR
    vp.add("mu_wd", mu[o:o + 32]); vp.add("mu_ad", mu[o + 32:o + 64]); vp.add("mu_gd", mu[o + 64:o + 160])
    if layer > 0:
        vp.add("mu_vd", mu[o + 160:o + 192])
    for j in range(4):
        sl = slice(j * 128, (j + 1) * 128)
        vp.add(f"w0{j}", inp["decay_w0"][layer][sl])
        vp.add(f"a0{j}", inp["iclr_a0"][layer][sl])
        vp.add(f"kk{j}", inp["k_k"][layer][sl])
        vp.add(f"ka{j}", inp["k_a"][layer][sl])
        vp.add(f"rk{j}", inp["r_k"][layer].reshape(-1)[sl])
        if layer > 0:
            vp.add(f"v0{j}", inp["vres_v0"][layer - 1][sl])
        vp.add(f"cb{j}", inp["cconv_b"][layer][sl])
        vp.add(f"cg{j}", inp["cln_g"][layer][sl])
        vp.add(f"cbb{j}", inp["cln_b"][layer][sl])
        for t in range(CW):
            vp.add(f"cw{j}_{t}", inp["cconv_w"][layer][t][sl])
    return vp


def _lora_up(layer, inp):
    a = np.zeros((4, 96, DR), np.float32)
    a[0, :32] = inp["decay_up"][layer]
    a[1, :32] = inp["iclr_up"][layer]
    a[2, :96] = inp["gate_up"][layer]
    if layer > 0:
        a[3, :32] = inp["vres_up"][layer - 1]
    return a


def build_A(layer, vidx):
    has_v = layer > 0
    COLS = 2752 if has_v else 2720
    nc = bass.Bass("TRN2", target_bir_lowering=False)
    dt_in = lambda n, s: nc.dram_tensor(n, list(s), F32, kind="ExternalInput").ap()
    dt_out = lambda n, s: nc.dram_tensor(n, list(s), F32, kind="ExternalOutput").ap()
    xT = dt_in("xT", [D, HALO + NT])
    w_in = dt_in("w_in", [D, COLS])
    pvec_d = dt_in("pvec", [128, len(vidx)])
    lup_d = dt_in("lup", [4, 96, DR])
    cA_d = dt_in("cA", [128, CA_W])
    rmask_d = dt_in("rmask", [128, NP])
    if has_v:
        vfin = dt_in("vfin", [DR, NT])
    else:
        vfout = dt_out("vfout", [DR, NT])
    yz = dt_out("yz", [8, 128, NT])
    psout = dt_out("psout", [8, 64, 128])
    bonus = dt_out("bonus", [DR, NT])
    gg = dt_out("gg", [DR, NT])
    cout = dt_out("cout", [DR, NT])

    with ExitStack() as es:
        S = Sched(nc, es)
        O = Ops(S)
        V = lambda name: pvec[:, vidx[name]:vidx[name] + 1]
        Vp = lambda name, m: pvec[0:m, vidx[name]:vidx[name] + 1]

        pvec = S.sb("pvec", [128, len(vidx)])
        cA = S.sb("cA", [128, CA_W])
        rmask = S.sb("rmask", [128, NP], BF16)
        lup = S.sb("lup", [96, 4, DR], BF16)
        cAb = S.sb("cAb", [128, 128], BF16)
        eps6 = S.sb("eps6", [128, 1]); eps5 = S.sb("eps5", [128, 1]); omka = S.sb("omka", [128, 4])
        S.dma("sync", pvec[:], pvec_d, writes=[pvec])
        S.dma("sync", cA[:], cA_d, writes=[cA])
        S.dma("gpsimd", rmask[:], rmask_d, writes=[rmask])
        S.dma("gpsimd", lup[:], lup_d.rearrange("f k n -> k f n"), writes=[lup])
        O.memset("gpsimd", eps6[:], 1e-6, [eps6])
        O.memset("gpsimd", eps5[:], 1e-5, [eps5])
        for j in range(4):
            O.ts("gpsimd", omka[:, j:j + 1], V(f"ka{j}"), -1.0, 1.0, ALU.mult, ALU.add, [pvec], [omka])
        ident = cA[:, CA_ID:CA_ID + 128]
        ones = cA[:, CA_ONES:CA_ONES + 128]
        blk = cA[:, CA_BLK:CA_BLK + 128]

        hT = S.sb("hT", [128, 8, PW], BF16)
        PL = [S.sb(f"pool{i}", [128, PW + 8]) for i in range(11)]
        twd = S.sb("twd", [32, NP], BF16); tad = S.sb("tad", [32, NP], BF16)
        sgd = S.sb("sgd", [96, NP], BF16); tvd = S.sb("tvd", [32, NP], BF16)
        AR = S.sb("AR", [128, 2, NP], BF16)
        BK = S.sb("BK", [128, 2, NP], BF16)
        NTT = NP // 128
        NCH = NP // 64
        TM = S.sb("TM", [128, NCH, 4, 64], BF16)
        VS = S.sb("VS", [128, NCH, 128], BF16)
        DG = S.sb("DG", [128, NCH, 64], BF16)
        ELC = S.sb("ELC", [128, NCH])
        STs = [S.sb(f"ST{j}", [128, 128], BF16) for j in range(4)]
        YZP = [S.sb(f"YZP{h}", [128, NP]) for h in range(2)]
        Xb = [[S.sb(f"X{s}{i}", [128, 2, 192], BF16) for i in range(2)] for s in range(2)]
        NKb = [S.sb(f"NK{s}", [128, 2, 128], BF16) for s in range(2)]
        NNb = [[S.sb(f"NN{s}{i}", [128, 2, 64], BF16) for i in range(2)] for s in range(2)]
        MKRs = [S.sb(f"MKR{s}", [128, 2, 64], BF16) for s in range(2)]
        G24s = [S.sb(f"G24{s}", [128, 2, 128], BF16) for s in range(2)]
        G13s = [S.sb(f"G13{s}", [128, 2, 128], BF16) for s in range(2)]
        wl = S.sb("wl", [128, 8, 224], BF16)
        WT = [S.sb(f"wt{i}", [128, 8, 128], BF16) for i in range(4)]
        wctr = [0]
        O.memset("gpsimd", VS[:], 0.0, [VS])
        for j in range(4):
            O.memset("gpsimd", STs[j][:], 0.0, [STs[j]])
            O.cp("gpsimd", STs[j][:, 64:128], cA[:, CA_ID64:CA_ID64 + 64], [cA], [STs[j]])

        psall = es.enter_context(nc.psum_tensor("psall", [128, 4096], F32))
        B = [Buf(f"B{i}", psall[:, i * 512:(i + 1) * 512]) for i in range(8)]
        for b_ in B:
            b_.excl = True
        PXs = [B[2], B[3]]
        MKp = [S.view(f"MK{s}", B[4 + s][:, 0:128], B[4 + s]) for s in range(2)]
        P2p = [S.view(f"P2{s}", B[4 + s][:, 128:256], B[4 + s]) for s in range(2)]
        PYp = [[S.view(f"PY{h}{s}", B[6 + h][:, s * 128:(s + 1) * 128], B[6 + h]) for s in range(2)] for h in range(2)]
        PSTp = [S.view(f"PST{h}", B[6 + h][h * 64:(h + 1) * 64, 256:384], B[6 + h]) for h in range(2)]
        BH = B[4]
        BLK = [(0, HALO), (HALO, HALO + 512), (HALO + 512, PW)]

        def fetch_w(col0):
            b = WT[wctr[0] % len(WT)]
            wctr[0] += 1
            S.dma("gpsimd", b[:], w_in[:, col0:col0 + 128].rearrange("(k p) n -> p k n", p=128), writes=[b])
            return b

        def project(wb, M, wcol, banks, halo_ap, halo_buf):
            outs = [(halo_ap, halo_buf), (banks[0][0:M, :], banks[0]), (banks[1][0:M, :], banks[1])]
            for (lo, hi), (oap, obuf) in zip(BLK, outs):
                for k in range(8):
                    O.mm(oap, wb[:, k, wcol:wcol + M], hT[:, k, lo:hi], [wb, hT], [obuf], start=(k == 0), stop=(k == 7))
            return outs

        def evac_shift(M, outs, mu_ap, dst_ap, dst_buf, pst, td):
            O.act(pst[0:M, 0:1], outs[0][0][:, HALO - 1:HALO], AF.Copy, [outs[0][1]], [pst])
            O.act(pst[0:M, 1:513], outs[1][0], AF.Copy, [outs[1][1]], [pst])
            O.act(pst[0:M, 513:1025], outs[2][0], AF.Copy, [outs[2][1]], [pst])
            O.tt("gpsimd", td[0:M, 0:NP], pst[0:M, 0:NP], pst[0:M, 1:NP + 1], ALU.subtract, [pst], [td])
            O.stt(dst_ap, td[0:M, 0:NP], mu_ap, pst[0:M, 1:NP + 1], ALU.mult, ALU.add, [td, pst, pvec], [dst_buf])

        def run_pass(p):
            E0 = p * NP
            own = slice(p * NP, (p + 1) * NP)
            xs = PL[0:8]
            for k in range(8):
                S.dma("sync", xs[k][:, 0:PW], xT[k * 128:(k + 1) * 128, E0:E0 + PW], writes=[xs[k]])
            sq = [PL[8], PL[9]]
            nouts = [(BH[:, 0:HALO], BH), (B[0][:, :], B[0]), (B[1][:, :], B[1])]
            for k in range(8):
                sb_ = sq[k % 2]
                O.act(sb_[:, 0:PW], xs[k][:, 0:PW], AF.Square, [xs[k]], [sb_])
                for (lo, hi), (oap, obuf) in zip(BLK, nouts):
                    O.mm(oap, ones, sb_[:, lo:hi], [cA, sb_], [obuf], start=(k == 0), stop=(k == 7))
            rs = PL[10]
            for (lo, hi), (oap, obuf) in zip(BLK, nouts):
                O.act(rs[:, lo:hi], oap, AF.Sqrt, [obuf, eps6], [rs], bias=eps6[:], scale=1.0 / D)
            O.recip(rs[:, 0:PW], rs[:, 0:PW], [rs], [rs])
            for k in range(8):
                O.stt(hT[:, k, :], xs[k][:, 0:PW], V(f"g{k}"), rs[:, 0:PW], ALU.mult, ALU.mult, [xs[k], rs, pvec], [hT])

            _chk("norm")
            nl = COLS - 2560
            S.dma("gpsimd", wl[:, :, 0:nl], w_in[:, 2560:COLS].rearrange("(k p) n -> p k n", p=128), writes=[wl])
            lspec = [("wd", 0, 32, AF.Tanh, twd), ("ad", 32, 32, AF.Copy, tad), ("gd", 64, 96, AF.Sigmoid, sgd)]
            if has_v:
                lspec.append(("vd", 160, 32, AF.Copy, tvd))
            for li, (nm, off, M, fn, dst) in enumerate(lspec):
                bk = (B[0], B[1]) if li % 2 == 0 else (B[2], B[3])
                hoff = (li % 2) * HALO
                outs = project(wl, M, off, bk, BH[0:M, hoff:hoff + HALO], BH)
                pst, td, q = PL[0], PL[1], PL[2]
                evac_shift(M, outs, Vp("mu_" + nm, M), q[0:M, 0:NP], q, pst, td)
                O.act(dst[0:M, :], q[0:M, 0:NP], fn, [q], [dst])

            _chk("lora")
            for j in range(4):
                jt = slice(j * 128, (j + 1) * 128)
                wk = fetch_w(1024 + DR + j * 128)
                wr = fetch_w(1024 + j * 128)
                wv = fetch_w(1024 + 2 * DR + j * 128)
                pst, td = PL[0], PL[1]
                AH, KRAW, T3, T4, RR, VV, T5, T6 = PL[2], PL[3], PL[4], PL[5], PL[6], PL[7], PL[8], PL[9]
                c2 = [(0, 512, B[0]), (512, 1024, B[1])]
                c2b = [(0, 512, B[2]), (512, 1024, B[3])]
                for lo, hi, bk in c2b:
                    O.mm(bk[:, :], lup[0:32, 1, jt], tad[0:32, lo:hi], [lup, tad], [bk])
                    O.act(AH[:, lo:hi], bk[:, :], AF.Sigmoid, [bk, pvec], [AH], bias=V(f"a0{j}"))
                outs = project(wk, 128, 0, (B[0], B[1]), BH[:, 0:HALO], BH)
                evac_shift(128, outs, V(f"mu_k{j}"), KRAW[:, 0:NP], KRAW, pst, td)
                O.ts("gpsimd", T3[:, 0:NP], KRAW[:, 0:NP], V(f"kk{j}"), None, ALU.mult, None, [KRAW, pvec], [T3])
                O.tt("gpsimd", T4[:, 0:NP], T3[:, 0:NP], T3[:, 0:NP], ALU.mult, [T3], [T4])
                for lo, hi, bk in c2b:
                    O.mm(bk[:, :], blk, T4[:, lo:hi], [cA, T4], [bk])
                    O.act(T5[:, lo:hi], bk[:, :], AF.Sqrt, [bk], [T5])
                O.ts("vector", T5[:, 0:NP], T5[:, 0:NP], 1e-12, None, ALU.max, None, [T5], [T5])
                O.recip(T5[:, 0:NP], T5[:, 0:NP], [T5], [T5])
                KK = T3
                O.tt("gpsimd", KK[:, 0:NP], T3[:, 0:NP], T5[:, 0:NP], ALU.mult, [T3, T5], [KK])
                O.ts("gpsimd", T4[:, 0:NP], AH[:, 0:NP], V(f"ka{j}"), omka[:, j:j + 1], ALU.mult, ALU.add, [AH, pvec, omka], [T4])
                KP = KRAW
                O.tt("gpsimd", KP[:, 0:NP], KRAW[:, 0:NP], T4[:, 0:NP], ALU.mult, [KRAW, T4], [KP])
                BF_ = AH
                O.tt("gpsimd", BF_[:, 0:NP], KK[:, 0:NP], AH[:, 0:NP], ALU.mult, [KK, AH], [BF_])
                outs = project(wr, 128, 0, (B[0], B[1]), BH[:, HALO:2 * HALO], BH)
                evac_shift(128, outs, V(f"mu_r{j}"), RR[:, 0:NP], RR, pst, td)
                O.stt(T4[:, 0:NP], RR[:, 0:NP], V(f"rk{j}"), KP[:, 0:NP], ALU.mult, ALU.mult, [RR, KP, pvec], [T4])
                RKS = T5
                for lo, hi, bk in c2b:
                    O.mm(bk[:, :], blk, T4[:, lo:hi], [cA, T4], [bk])
                    O.act(RKS[:, lo:hi], bk[:, :], AF.Copy, [bk], [RKS])
                outs = project(wv, 128, 0, (B[0], B[1]), BH[:, 0:HALO], BH)
                evac_shift(128, outs, V(f"mu_v{j}"), VV[:, 0:NP], VV, pst, td)
                if has_v:
                    for lo, hi, bk in c2b:
                        O.mm(bk[:, :], lup[0:32, 3, jt], tvd[0:32, lo:hi], [lup, tvd], [bk])
                        O.act(T4[:, lo:hi], bk[:, :], AF.Sigmoid, [bk, pvec], [T4], bias=V(f"v0{j}"))
                    S.dma("sync", T6[:, 0:NP], vfin[jt, own], writes=[T6])
                    O.tt("gpsimd", T6[:, 0:NP], T6[:, 0:NP], VV[:, 0:NP], ALU.subtract, [T6, VV], [T6])
                    O.tt("gpsimd", T6[:, 0:NP], T6[:, 0:NP], T4[:, 0:NP], ALU.mult, [T6, T4], [T6])
                    O.tt("gpsimd", VV[:, 0:NP], VV[:, 0:NP], T6[:, 0:NP], ALU.add, [VV, T6], [VV])
                else:
                    S.dma("sync", vfout[jt, own], VV[:, 0:NP], reads=[VV], store=True)
                O.tt("gpsimd", T4[:, 0:NP], RKS[:, 0:NP], VV[:, 0:NP], ALU.mult, [RKS, VV], [T4])
                S.dma("sync", bonus[jt, own], T4[:, 0:NP], reads=[T4], store=True)
                for lo, hi, bk in c2b:
                    O.mm(bk[:, :], lup[0:96, 2, jt], sgd[0:96, lo:hi], [lup, sgd], [bk])
                    O.act(T6[:, lo:hi], bk[:, :], AF.Copy, [bk], [T6])
                S.dma("sync", gg[jt, own], T6[:, 0:NP], reads=[T6], store=True)
                LW, LL = PL[10], T5
                for lo, hi, bk in c2b:
                    O.mm(bk[:, :], lup[0:32, 0, jt], twd[0:32, lo:hi], [lup, twd], [bk])
                    O.act(LW[:, lo:hi], bk[:, :], AF.Sigmoid, [bk, pvec], [LW], bias=V(f"w0{j}"))
                O.ts("gpsimd", LW[:, 0:NP], LW[:, 0:NP], -0.6065306597126334, None, ALU.mult, None, [LW], [LW])
                O.scan(LL[:, 0:NP], rmask[:, :], LW[:, 0:NP], [rmask, LW], [LL])
                E1, E2 = pst, td
                O.act(E1[:, 0:NP], LL[:, 0:NP], AF.Exp, [LL], [E1])
                O.act(E2[:, 0:NP], LL[:, 0:NP], AF.Exp, [LL], [E2], scale=-1.0)
                O.tt("gpsimd", AR[:, 1, :], RR[:, 0:NP], E1[:, 0:NP], ALU.mult, [RR, E1], [AR])
                O.cp("gpsimd", ELC[:, :], E1[:, 0:NP].rearrange("p (c t) -> p c t", t=64)[:, :, 63], [E1], [ELC])
                elc_b = ELC[:, :].unsqueeze(2).to_broadcast([128, NCH, 64])
                O.tt("vector", DG[:, :, :], cA[:, CA_ID64:CA_ID64 + 64].unsqueeze(1).to_broadcast([128, NCH, 64]), elc_b, ALU.mult, [cA, ELC], [DG])
                v3 = lambda b_: b_[:, 0:NP].rearrange("p (c t) -> p c t", t=64)
                O.tt("gpsimd", BF_[:, 0:NP], BF_[:, 0:NP], E2[:, 0:NP], ALU.mult, [BF_, E2], [BF_])
                O.cp("scalar", BK[:, 0, :], BF_[:, 0:NP], [BF_], [BK])
                O.tt("gpsimd", v3(BF_), v3(BF_), elc_b, ALU.mult, [BF_, ELC], [BF_])
                O.tt("gpsimd", KP[:, 0:NP], KP[:, 0:NP], E2[:, 0:NP], ALU.mult, [KP, E2], [KP])
                O.cp("scalar", BK[:, 1, :], KP[:, 0:NP], [KP], [BK])
                O.tt("gpsimd", v3(KP), v3(KP), elc_b, ALU.mult, [KP, ELC], [KP])
                O.tt("gpsimd", LW[:, 0:NP], LL[:, 0:NP], LW[:, 0:NP], ALU.subtract, [LL, LW], [LW])
                O.act(LW[:, 0:NP], LW[:, 0:NP], AF.Exp, [LW], [LW])
                O.stt(KK[:, 0:NP], KK[:, 0:NP], -1.0, LW[:, 0:NP], ALU.mult, ALU.mult, [KK, LW], [KK])
                O.cp("scalar", AR[:, 0, :], KK[:, 0:NP], [KK], [AR])
                _chk("prep")
                for c0 in range(0, NCH, 2):
                    Bt = B[4 + (c0 // 2) % 2]
                    for ci in range(2):
                        c = c0 + ci
                        for qi, src in enumerate((KK, BF_, KP, VV)):
                            for h in range(2):
                                hp = slice(h * 64, (h + 1) * 64)
                                O.mm(Bt[hp, ci * 256 + qi * 64:ci * 256 + (qi + 1) * 64], src[hp, c * 64:(c + 1) * 64],
                                     cA[hp, CA_ID + h * 64:CA_ID + (h + 1) * 64], [src, cA], [Bt])
                    O.cp("scalar", TM[:, c0:c0 + 2, :, :], Bt[:, :].rearrange("p (c q n) -> p c q n", q=4, n=64), [Bt], [TM])
                    O.cp("scalar", VS[:, c0:c0 + 2, 0:64], Bt[:, :].rearrange("p (c q n) -> p c q n", q=4, n=64)[:, :, 3, :], [Bt], [VS])
                _chk("trans")

                ST = STs[j]
                m_ab = cA[:, CA_AB:CA_AB + 128].unsqueeze(1).to_broadcast([128, 2, 128])
                m_nk = cA[:, CA_NK:CA_NK + 128].unsqueeze(1).to_broadcast([128, 2, 128])
                m_kr = cA[:, CA_KR:CA_KR + 64].unsqueeze(1).to_broadcast([128, 2, 64])

                def stages(tau):
                    s = tau % 2
                    Bq, PX, MK, P2 = B[s], PXs[s], MKp[s], P2p[s]
                    X, NK, NN, MKR, G24, G13 = Xb[s], NKb[s], NNb[s], MKRs[s], G24s[s], G13s[s]
                    PR = [slice(0, 64), slice(64, 128)]

                    def cc(par):
                        c = 2 * tau + par
                        return slice(c * 64, (c + 1) * 64)

                    def st_abc():
                        for par in range(2):
                            for h in range(2):
                                hp = PR[h]
                                O.mm(Bq[hp, par * 128:(par + 1) * 128], BK[hp, 0, cc(par)], AR[hp, :, cc(par)], [BK, AR], [Bq])
                                O.mm(Bq[hp, 256 + par * 128:256 + (par + 1) * 128], AR[hp, 0, cc(par)], BK[hp, :, cc(par)], [BK, AR], [Bq])
                                O.mm(MK[hp, par * 64:(par + 1) * 64], BK[hp, 1, cc(par)], AR[hp, 1, cc(par)], [BK, AR], [MK])

                    def st_ev0():
                        O.tt("vector", X[0][:, :, 0:128], Bq[:, 0:256].rearrange("p (c n) -> p c n", n=128), m_ab, ALU.mult, [Bq, cA], [X[0]])
                        O.cp("gpsimd", X[0][:, :, 128:192], TM[:, 2 * tau:2 * tau + 2, 1, :], [TM], [X[0]])
                        O.tt("vector", NK[:, :, :], Bq[:, 256:512].rearrange("p (c n) -> p c n", n=128), m_nk, ALU.mult, [Bq, cA], [NK])
                        O.tt("vector", MKR[:, :, :], MK[:, :].rearrange("p (c n) -> p c n", n=64), m_kr, ALU.mult, [MK, cA], [MKR])

                    def st_lvl(lv):
                        def f():
                            Xi, Xo = X[lv % 2], X[(lv + 1) % 2]
                            Ncur = NK if lv == 0 else NN[lv % 2]
                            for par in range(2):
                                for h in range(2):
                                    hp = PR[h]
                                    nl_ = Ncur[hp, par, 0:64]
                                    if lv < 5:
                                        O.mm(PX[hp, par * 256:par * 256 + 192], nl_, Xi[hp, par, 0:192], [Ncur, Xi], [PX])
                                        O.mm(P2[hp, par * 64:(par + 1) * 64], Xi[hp, par, 0:64], nl_, [Ncur, Xi], [P2])
                                    else:
                                        O.mm(PX[hp, par * 256 + 64:par * 256 + 192], nl_, Xi[hp, par, 64:192], [Ncur, Xi], [PX])
                            pxv = PX[:, :].rearrange("p (c n) -> p c n", n=256)
                            if lv < 5:
                                O.cp("scalar", Xo[:, :, 0:64], pxv[:, :, 0:64], [PX], [Xo])
                                O.cp("scalar", NN[(lv + 1) % 2][:, :, :], P2[:, :].rearrange("p (c n) -> p c n", n=64), [P2], [NN[(lv + 1) % 2]])
                            O.tt("vector", Xo[:, :, 64:192], pxv[:, :, 64:192], Xi[:, :, 64:192], ALU.add, [PX, Xi], [Xo])
                        return f

                    def st_fin():
                        W = X[0]
                        for par in range(2):
                            for h in range(2):
                                hp = PR[h]
                                O.mm(Bq[hp, par * 128:(par + 1) * 128], NK[hp, par, 64:128], W[hp, par, 64:192], [NK, W], [Bq])
                                O.mm(Bq[hp, 256 + par * 128:256 + (par + 1) * 128], TM[hp, 2 * tau + par, 0, :], W[hp, par, 64:192], [TM, W], [Bq])

                    def st_evf():
                        g24p = Bq[:, 0:256].rearrange("p (c n) -> p c n", n=128)
                        g13p = Bq[:, 256:512].rearrange("p (c n) -> p c n", n=128)
                        O.tt("vector", G24[:, :, 0:64], g24p[:, :, 0:64], MKR[:, :, :], ALU.add, [Bq, MKR], [G24])
                        O.tt("vector", G24[:, :, 64:128], g24p[:, :, 64:128], TM[:, 2 * tau:2 * tau + 2, 2, :], ALU.add, [Bq, TM], [G24])
                        O.tt("vector", G13[:, :, 0:64], g13p[:, :, 0:64], AR[:, 1, tau * 128:(tau + 1) * 128].rearrange("p (c n) -> p c n", n=64), ALU.add, [Bq, AR], [G13])
                        O.tt("vector", G13[:, :, 64:128], g13p[:, :, 64:128], DG[:, 2 * tau:2 * tau + 2, :], ALU.add, [Bq, DG], [G13])

                    def st_seq():
                        for par in range(2):
                            c = 2 * tau + par
                            for h in range(2):
                                hp = PR[h]
                                yo = PYp[h][s][:, par * 64:(par + 1) * 64]
                                O.mm(yo, ST[hp, :], G13[hp, par, 0:64], [ST, G13], [PYp[h][s]], start=True, stop=False)
                                O.mm(yo, VS[hp, c, :], G24[hp, par, 0:64], [VS, G24], [PYp[h][s]], start=False, stop=True)
                            for h in range(2):
                                hp = PR[h]
                                O.mm(PSTp[h][:, :], G13[hp, par, 64:128], ST[hp, :], [ST, G13], [PSTp[h]], start=True, stop=False)
                                O.mm(PSTp[h][:, :], G24[hp, par, 64:128], VS[hp, c, :], [VS, G24], [PSTp[h]], start=False, stop=True)
                            for h in range(2):
                                O.cp("scalar", ST[PR[h], :], PSTp[h][:, :], [PSTp[h]], [ST])
                        for h in range(2):
                            O.cp("scalar", YZP[h][:, tau * 128:(tau + 1) * 128], PYp[h][s][:, :], [PYp[h][s]], [YZP[h]])

                    return [st_abc, st_ev0] + [st_lvl(lv) for lv in range(6)] + [st_fin, st_evf, st_seq]

                allst = [stages(tau) for tau in range(NTT)]
                nst = len(allst[0])
                SK = 5
                for t in range(SK * (NTT - 1) + nst):
                    for tau in range(NTT):
                        k = t - tau * SK
                        if 0 <= k < nst:
                            allst[tau][k]()
                for h in range(2):
                    S.dma("sync", yz[2 * j + h, :, own], YZP[h][:, :], reads=[YZP[h]], store=True)

            _chk("scan")
            CV = [PL[3], PL[4], PL[5], PL[6]]
            for i in range(4):
                w1 = fetch_w(i * 128)
                w2 = fetch_w(DR + i * 128)
                o1 = project(w1, 128, 0, (B[0], B[1]), BH[:, 0:HALO], BH)
                o2 = project(w2, 128, 0, (B[2], B[3]), BH[:, HALO:2 * HALO], BH)
                SG, CI = PL[0], PL[1]
                for (lo, hi), (a2, b2), (a1, b1) in zip(BLK, o2, o1):
                    O.act(SG[:, lo:hi], a2, AF.Sigmoid, [b2], [SG])
                    O.tt("vector", CI[:, lo:hi], a1, SG[:, lo:hi], ALU.mult, [b1, SG], [CI])
                acc = CV[i]
                O.ts("gpsimd", acc[:, 0:NP], CI[:, 2:2 + NP], V(f"cw{i}_0"), V(f"cb{i}"), ALU.mult, ALU.add, [CI, pvec], [acc])
                for t in range(1, CW):
                    O.stt(acc[:, 0:NP], CI[:, 2 + t:2 + t + NP], V(f"cw{i}_{t}"), acc[:, 0:NP], ALU.mult, ALU.add, [CI, acc, pvec], [acc])
            c2 = [(0, 512, B[0], B[2]), (512, 1024, B[1], B[3])]
            for i in range(4):
                SQ = PL[i % 2]
                O.act(SQ[:, 0:NP], CV[i][:, 0:NP], AF.Square, [CV[i]], [SQ])
                for lo, hi, b1, b2 in c2:
                    O.mm(b1[:, :], ones, CV[i][:, lo:hi], [cA, CV[i]], [b1], start=(i == 0), stop=(i == 3))
                    O.mm(b2[:, :], ones, SQ[:, lo:hi], [cA, SQ], [b2], start=(i == 0), stop=(i == 3))
            ME, VR, T = PL[7], PL[8], PL[9]
            for lo, hi, b1, b2 in c2:
                O.act(ME[:, lo:hi], b1[:, :], AF.Copy, [b1], [ME], scale=1.0 / DR)
            O.tt("gpsimd", T[:, 0:NP], ME[:, 0:NP], ME[:, 0:NP], ALU.mult, [ME], [T])
            for lo, hi, b1, b2 in c2:
                O.stt(VR[:, lo:hi], b2[:, :], 1.0 / DR, T[:, lo:hi], ALU.mult, ALU.subtract, [b2, T], [VR])
            O.act(VR[:, 0:NP], VR[:, 0:NP], AF.Sqrt, [VR, eps5], [VR], bias=eps5[:])
            O.recip(VR[:, 0:NP], VR[:, 0:NP], [VR], [VR])
            for i in range(4):
                T1 = PL[i % 2]
                O.tt("gpsimd", T1[:, 0:NP], CV[i][:, 0:NP], ME[:, 0:NP], ALU.subtract, [CV[i], ME], [T1])
                O.tt("gpsimd", T1[:, 0:NP], T1[:, 0:NP], VR[:, 0:NP], ALU.mult, [T1, VR], [T1])
                O.ts("gpsimd", T1[:, 0:NP], T1[:, 0:NP], V(f"cg{i}"), V(f"cbb{i}"), ALU.mult, ALU.add, [T1, pvec], [T1])
                O.act(T1[:, 0:NP], T1[:, 0:NP], AF.Silu, [T1], [T1])
                S.dma("sync", cout[i * 128:(i + 1) * 128, own], T1[:, 0:NP], reads=[T1], store=True)

        try:
            for p in range(NPASS):
                run_pass(p)
                _chk("pass0")
        except _Stop:
            pass
        for j in range(4):
            STF, PSO = PL[0], PL[1]
            O.cp("vector", STF[:, 0:128], STs[j][:, :], [STs[j]], [STF])
            for h in range(2):
                hp = slice(h * 64, (h + 1) * 64)
                O.mm(B[7][hp, 0:64], STF[hp, 64:128], cA[hp, CA_ID + h * 64:CA_ID + (h + 1) * 64], [STF, cA], [B[7]])
            O.cp("vector", PSO[:, 0:64], STF[:, 0:64], [STF], [PSO])
            O.cp("vector", PSO[:, 64:128], B[7][:, 0:64], [B[7]], [PSO])
            S.dma("sync", psout[2 * j:2 * j + 2].rearrange("h k n -> (h k) n"), PSO[:, 0:128], reads=[PSO], store=True)
        S.finish()
        S.emit()
    return nc


HB = 2
NB = HB + NP


def _pack_B(layer, inp):
    vp = VecPack()
    for j in range(4):
        sl = slice(j * 128, (j + 1) * 128)
        vp.add(f"lg{j}", inp["lnx_g"][layer][sl])
        vp.add(f"lb{j}", inp["lnx_b"][layer][sl])
    for m in range(8):
        sl = slice(m * 128, (m + 1) * 128)
        vp.add(f"gpm{m}", inp["norm_post_mix"][layer][sl])
        vp.add(f"gpf{m}", inp["norm_pre_ffn"][layer][sl])
        vp.add(f"gqf{m}", inp["norm_post_ffn"][layer][sl])
    for i in range(2 * NFF):
        sl = slice(i * 128, (i + 1) * 128)
        for t in range(3):
            vp.add(f"fw{i}_{t}", inp["ffn_conv_w"][layer][t][sl])
        vp.add(f"fb{i}", inp["ffn_conv_b"][layer][sl])
    return vp


def build_B(layer, vidx):
    nc = bass.Bass("TRN2", target_bir_lowering=False)
    dt_in = lambda n, s: nc.dram_tensor(n, list(s), F32, kind="ExternalInput").ap()
    yzh = dt_in("yzh", [8, 128, HB + NT])
    bonus = dt_in("bonus", [DR, HB + NT])
    gg = dt_in("gg", [DR, HB + NT])
    cout = dt_in("cout", [DR, HB + NT])
    xTh = dt_in("xTh", [D, HB + NT])
    psq = dt_in("psq", [3, 8, 64, 128])
    w_out = dt_in("w_out", [D, D])
    ffn_up = dt_in("ffn_up", [D, 2 * DFF])
    ffn_down = dt_in("ffn_down", [DFF, D])
    pvec_d = dt_in("pvec", [128, len(vidx)])
    cA_d = dt_in("cA", [128, CA_W])
    outT = nc.dram_tensor("outT", [D, NT], F32, kind="ExternalOutput").ap()

    with ExitStack() as es:
        S = Sched(nc, es)
        O = Ops(S)
        V = lambda name: pvec[:, vidx[name]:vidx[name] + 1]
        pvec = S.sb("pvec", [128, len(vidx)])
        cA = S.sb("cA", [128, CA_W])
        eps6 = S.sb("eps6", [128, 1]); epsg = S.sb("epsg", [128, 1])
        S.dma("sync", pvec[:], pvec_d, writes=[pvec])
        S.dma("sync", cA[:], cA_d, writes=[cA])
        O.memset("gpsimd", eps6[:], 1e-6, [eps6])
        O.memset("gpsimd", epsg[:], 64e-5, [epsg])
        ones = cA[:, CA_ONES:CA_ONES + 128]
        blk = cA[:, CA_BLK:CA_BLK + 128]

        XM = S.sb("XM", [128, 8, NB])
        MF = S.sb("MF", [128, 8, NB], BF16)
        MH = S.sb("MH", [128, 8, NB], BF16)
        ACTB = S.sb("ACTB", [128, NFF, NP], BF16)
        PL = [S.sb(f"pool{i}", [128, NB + 6]) for i in range(8)]
        WT = [S.sb(f"wt{i}", [128, 8, 128], BF16) for i in range(6)]
        WD = [S.sb(f"wd{i}", [128, NFF, 128], BF16) for i in range(2)]
        PQ = S.sb("PQ", [128, 3, 4, 128])
        STc = S.sb("STc", [128, 4, 64]); STp = S.sb("STp", [128, 4, 64])
        wctr = [0]

        psall = es.enter_context(nc.psum_tensor("psall", [128, 4096], F32))
        B = [Buf(f"B{i}", psall[:, i * 512:(i + 1) * 512]) for i in range(8)]
        for b_ in B:
            b_.excl = True
        BH = B[4]
        BLB = [(0, HB), (HB, HB + 512), (HB + 512, NB)]
        PR = [slice(0, 64), slice(64, 128)]

        def fetch_w(src_ap):
            b = WT[wctr[0] % len(WT)]
            wctr[0] += 1
            S.dma("gpsimd", b[:], src_ap.rearrange("(k p) n -> p k n", p=128), writes=[b])
            return b

        S.dma("sync", PQ[:], psq.rearrange("i (j h) k n -> (h k) i j n", h=2), writes=[PQ])
        O.memset("vector", STc[:], 0.0, [STc])
        for i in range(3):
            for j in range(4):
                for h in range(2):
                    O.mm(B[0][PR[h], j * 64:(j + 1) * 64], PQ[PR[h], i, j, 64:128], STc[PR[h], j, :], [PQ, STc], [B[0]])
            O.tt("vector", STc[:, :, :], B[0][:, 0:256].rearrange("p (j n) -> p j n", n=64), PQ[:, i, :, 0:64], ALU.add, [B[0], PQ], [STc])
            if i == 1:
                O.cp("vector", STp[:], STc[:], [STc], [STp])

        def stats_rstd(acc, dst, scale, eps_t):
            for ap_, buf_, lo, hi in acc:
                O.act(dst[:, lo:hi], ap_, AF.Sqrt, [buf_, eps_t], [dst], bias=eps_t[:], scale=scale)
            O.recip(dst[:, 0:NB], dst[:, 0:NB], [dst], [dst])

        for p in range(NPASS):
            c0 = p * NP
            own = slice(p * NP, (p + 1) * NP)
            cols = slice(c0, c0 + NB)
            S.dma("sync", XM[:], xTh[:, cols].rearrange("(m q) t -> q m t", q=128), writes=[XM])
            for j in range(4):
                YL, Z, Y, T1, T2, T3, BO, GG = PL
                for h in range(2):
                    S.dma("sync", YL[PR[h], 0:NB], yzh[2 * j + h, 0:64, cols], writes=[YL])
                    S.dma("sync", Z[PR[h], 0:NB], yzh[2 * j + h, 64:128, cols], writes=[Z])
                S.dma("sync", BO[:, 0:NB], bonus[j * 128:(j + 1) * 128, cols], writes=[BO])
                S.dma("sync", GG[:, 0:NB], gg[j * 128:(j + 1) * 128, cols], writes=[GG])
                youts = [(BH[:, 0:HB], BH), (B[0][:, :], B[0]), (B[1][:, :], B[1])]
                for bi, ((lo, hi), (oap, obuf)) in enumerate(zip(BLB, youts)):
                    stt_ = STp if (bi == 0 and p == 0) else STc
                    for h in range(2):
                        O.mm(oap[PR[h], :], stt_[PR[h], j, :], Z[PR[h], lo:hi], [stt_, Z], [obuf])
                    O.tt("vector", Y[:, lo:hi], oap, YL[:, lo:hi], ALU.add, [obuf, YL], [Y])
                O.act(T1[:, 0:NB], Y[:, 0:NB], AF.Square, [Y], [T1])
                s1 = [(BH[:, 32:32 + HB], BH), (B[2][:, :], B[2]), (B[3][:, :], B[3])]
                s2 = [(BH[:, 64:64 + HB], BH), (B[5][:, :], B[5]), (B[6][:, :], B[6])]
                for (lo, hi), (a1, b1), (a2, b2) in zip(BLB, s1, s2):
                    O.mm(a1, blk, Y[:, lo:hi], [cA, Y], [b1])
                    O.mm(a2, blk, T1[:, lo:hi], [cA, T1], [b2])
                ME, VR = T2, T3
                for (lo, hi), (a1, b1), (a2, b2) in zip(BLB, s1, s2):
                    O.act(ME[:, lo:hi], a1, AF.Copy, [b1], [ME], scale=1.0 / 64)
                O.tt("gpsimd", T1[:, 0:NB], ME[:, 0:NB], ME[:, 0:NB], ALU.mult, [ME], [T1])
                for (lo, hi), (a1, b1), (a2, b2) in zip(BLB, s1, s2):
                    O.stt(VR[:, lo:hi], a2, 1.0 / 64, T1[:, lo:hi], ALU.mult, ALU.subtract, [b2, T1], [VR])
                O.act(VR[:, 0:NB], VR[:, 0:NB], AF.Sqrt, [VR, epsg], [VR], bias=epsg[:])
                O.recip(VR[:, 0:NB], VR[:, 0:NB], [VR], [VR])
                O.tt("gpsimd", Y[:, 0:NB], Y[:, 0:NB], ME[:, 0:NB], ALU.subtract, [Y, ME], [Y])
                O.tt("gpsimd", Y[:, 0:NB], Y[:, 0:NB], VR[:, 0:NB], ALU.mult, [Y, VR], [Y])
                O.ts("gpsimd", Y[:, 0:NB], Y[:, 0:NB], V(f"lg{j}"), V(f"lb{j}"), ALU.mult, ALU.add, [Y, pvec], [Y])
                O.tt("gpsimd", Y[:, 0:NB], Y[:, 0:NB], BO[:, 0:NB], ALU.add, [Y, BO], [Y])
                O.tt("gpsimd", MH[:, j, :], Y[:, 0:NB], GG[:, 0:NB], ALU.mult, [Y, GG], [MH])
            S.dma("gpsimd", MH[:, 4:8, :], cout[:, cols].rearrange("(i q) t -> q i t", q=128), writes=[MH])

            acc = [(BH[:, 0:HB], BH, 0, HB), (B[5][:, :], B[5], HB, HB + 512), (B[6][:, :], B[6], HB + 512, NB)]
            for m in range(8):
                wb = fetch_w(w_out[:, m * 128:(m + 1) * 128])
                bk = (B[0], B[1]) if m % 2 == 0 else (B[2], B[3])
                mouts = [(B[7][:, (m % 2) * 32:(m % 2) * 32 + HB], B[7]), (bk[0][:, :], bk[0]), (bk[1][:, :], bk[1])]
                SQ = PL[m % 2]
                for (lo, hi), (oap, obuf) in zip(BLB, mouts):
                    for k in range(8):
                        O.mm(oap, wb[:, k, :], MH[:, k, lo:hi], [wb, MH], [obuf], start=(k == 0), stop=(k == 7))
                    O.act(MF[:, m, lo:hi], oap, AF.Copy, [obuf], [MF])
                    O.act(SQ[:, lo:hi], oap, AF.Square, [obuf], [SQ])
                for (aap, abuf, lo, hi) in acc:
                    O.mm(aap, ones, SQ[:, lo:hi], [cA, SQ], [abuf], start=(m == 0), stop=(m == 7))
            RS = PL[2]
            stats_rstd(acc, RS, 1.0 / D, eps6)
            for m in range(8):
                T = PL[3 + m % 2]
                O.stt(T[:, 0:NB], MF[:, m, :], V(f"gpm{m}"), RS[:, 0:NB], ALU.mult, ALU.mult, [MF, RS, pvec], [T])
                O.tt("gpsimd", XM[:, m, :], XM[:, m, :], T[:, 0:NB], ALU.add, [XM, T], [XM])
            for m in range(8):
                SQ = PL[m % 2]
                O.act(SQ[:, 0:NB], XM[:, m, :], AF.Square, [XM], [SQ])
                for (aap, abuf, lo, hi) in acc:
                    O.mm(aap, ones, SQ[:, lo:hi], [cA, SQ], [abuf], start=(m == 0), stop=(m == 7))
            stats_rstd(acc, RS, 1.0 / D, eps6)
            for m in range(8):
                O.stt(MH[:, m, :], XM[:, m, :], V(f"gpf{m}"), RS[:, 0:NB], ALU.mult, ALU.mult, [XM, RS, pvec], [MH])
            for i in range(NFF):
                wg = fetch_w(ffn_up[:, i * 128:(i + 1) * 128])
                wu = fetch_w(ffn_up[:, DFF + i * 128:DFF + (i + 1) * 128])
                res_ = []
                for which, (wb, bk, ho) in enumerate(((wg, (B[0], B[1]), 0), (wu, (B[2], B[3]), 32))):
                    fi = i + which * NFF
                    outs = [(BH[:, ho:ho + HB], BH), (bk[0][:, :], bk[0]), (bk[1][:, :], bk[1])]
                    HG = PL[2 * which]
                    CG = PL[2 * which + 1]
                    for (lo, hi), (oap, obuf) in zip(BLB, outs):
                        for k in range(8):
                            O.mm(oap, wb[:, k, :], MH[:, k, lo:hi], [wb, MH], [obuf], start=(k == 0), stop=(k == 7))
                        O.act(HG[:, lo:hi], oap, AF.Copy, [obuf], [HG])
                    O.ts("gpsimd", CG[:, 0:NP], HG[:, 2:2 + NP], V(f"fw{fi}_2"), V(f"fb{fi}"), ALU.mult, ALU.add, [HG, pvec], [CG])
                    O.stt(CG[:, 0:NP], HG[:, 1:1 + NP], V(f"fw{fi}_1"), CG[:, 0:NP], ALU.mult, ALU.add, [HG, CG, pvec], [CG])
                    O.stt(CG[:, 0:NP], HG[:, 0:NP], V(f"fw{fi}_0"), CG[:, 0:NP], ALU.mult, ALU.add, [HG, CG, pvec], [CG])
                    res_.append(CG)
                O.act(res_[0][:, 0:NP], res_[0][:, 0:NP], AF.Gelu_apprx_tanh, [res_[0]], [res_[0]])
                O.tt("gpsimd", ACTB[:, i, :], res_[0][:, 0:NP], res_[1][:, 0:NP], ALU.mult, [res_[0], res_[1]], [ACTB])
            acc2 = [(B[5][:, :], B[5], 0, 512), (B[6][:, :], B[6], 512, NP)]
            for m in range(8):
                wd = WD[m % 2]
                S.dma("gpsimd", wd[:], ffn_down[:, m * 128:(m + 1) * 128].rearrange("(i q) n -> q i n", q=128), writes=[wd])
                bk = (B[0], B[1]) if m % 2 == 0 else (B[2], B[3])
                SQ = PL[m % 2]
                for bi, (lo, hi) in enumerate(((0, 512), (512, NP))):
                    for i in range(NFF):
                        O.mm(bk[bi][:, :], wd[:, i, :], ACTB[:, i, lo:hi], [wd, ACTB], [bk[bi]], start=(i == 0), stop=(i == NFF - 1))
                    O.act(MF[:, m, lo:hi], bk[bi][:, :], AF.Copy, [bk[bi]], [MF])
                    O.act(SQ[:, lo:hi], bk[bi][:, :], AF.Square, [bk[bi]], [SQ])
                for (aap, abuf, lo, hi) in acc2:
                    O.mm(aap, ones, SQ[:, lo:hi], [cA, SQ], [abuf], start=(m == 0), stop=(m == 7))
            for (aap, abuf, lo, hi) in acc2:
                O.act(RS[:, lo:hi], aap, AF.Sqrt, [abuf, eps6], [RS], bias=eps6[:], scale=1.0 / D)
            O.recip(RS[:, 0:NP], RS[:, 0:NP], [RS], [RS])
            for m in range(8):
                T = PL[3 + m % 4]
                O.stt(T[:, 0:NP], MF[:, m, 0:NP], V(f"gqf{m}"), RS[:, 0:NP], ALU.mult, ALU.mult, [MF, RS, pvec], [T])
                O.tt("gpsimd", T[:, 0:NP], T[:, 0:NP], XM[:, m, HB:NB], ALU.add, [T, XM], [T])
                S.dma("sync", outT[m * 128:(m + 1) * 128, own], T[:, 0:NP], reads=[T], store=True)
        S.finish()
        S.emit()
    return nc


_CACHE = {}


def _get(name, fn):
    if name not in _CACHE:
        _CACHE[name] = fn()
    return _CACHE[name]


def run_A(layer, xT_ext, inp, vfin=None):
    vp = _pack_A(layer, inp)
    nc = _get(("A", layer), lambda: build_A(layer, vp.idx))
    cs = _consts()
    w_in = np.ascontiguousarray(inp["w_in_first"] if layer == 0 else inp["w_in_rest"][layer - 1])
    pv = vp.array()
    lup = _lora_up(layer, inp)
    maps = []
    for c in range(NCORES):
        m = {"xT": xT_ext[c], "w_in": w_in, "pvec": pv, "lup": lup, "cA": cs["cA"], "rmask": cs["rmask"]}
        if layer > 0:
            m["vfin"] = vfin[c]
        maps.append(m)
    res = run_bass_kernel_spmd(nc, maps, core_ids=list(range(NCORES)))
    return res.results


def run_B(layer, bin_list, inp):
    vp = _pack_B(layer, inp)
    nc = _get(("B", layer), lambda: build_B(layer, vp.idx))
    cs = _consts()
    pv = vp.array()
    w_out = np.ascontiguousarray(inp["w_out"][layer])
    ffn_up = np.ascontiguousarray(inp["ffn_up"][layer])
    ffn_down = np.ascontiguousarray(inp["ffn_down"][layer])
    maps = []
    for c in range(NCORES):
        m = dict(bin_list[c])
        m.update({"w_out": w_out, "ffn_up": ffn_up, "ffn_down": ffn_down, "pvec": pv, "cA": cs["cA"]})
        maps.append(m)
    res = run_bass_kernel_spmd(nc, maps, core_ids=list(range(NCORES)))
    return res.results


def _with_halo(arrs, width):
    out = []
    for c in range(NCORES):
        a = arrs[c]
        if c % 4 == 0:
            hl = np.zeros(a.shape[:-1] + (width,), a.dtype)
        else:
            hl = arrs[c - 1][..., -width:]
        out.append(np.ascontiguousarray(np.concatenate([hl, a], axis=-1)))
    return out


def _psq(ps_list):
    ident = np.zeros((8, 64, 128), np.float32)
    ident[:, :, 64:] = np.eye(64, dtype=np.float32)[None]
    out = []
    for c in range(NCORES):
        seg = c % 4
        sl = []
        for i in range(3):
            src = seg - 3 + i
            sl.append(ps_list[c - seg + src] if src >= 0 else ident)
        out.append(np.ascontiguousarray(np.stack(sl, 0)))
    return out


def kernel(**inputs):
    inp = {k: np.asarray(v) for k, v in inputs.items()}
    x = inp["x"]
    xT = []
    for c in range(NCORES):
        b, sg = c // 4, c % 4
        xT.append(np.ascontiguousarray(x[b, sg * NT:(sg + 1) * NT, :].T))
    vfirst = None
    for layer in range(2):
        resA = run_A(layer, _with_halo(xT, HALO), inp, vfirst)
        if layer == 0:
            vfirst = [resA[c]["vfout"] for c in range(NCORES)]
        yzh = _with_halo([resA[c]["yz"] for c in range(NCORES)], HB)
        boh = _with_halo([resA[c]["bonus"] for c in range(NCORES)], HB)
        ggh = _with_halo([resA[c]["gg"] for c in range(NCORES)], HB)
        coh = _with_halo([resA[c]["cout"] for c in range(NCORES)], HB)
        xh = _with_halo(xT, HB)
        psq = _psq([resA[c]["psout"] for c in range(NCORES)])
        bins = [{"yzh": yzh[c], "bonus": boh[c], "gg": ggh[c], "cout": coh[c], "xTh": xh[c], "psq": psq[c]} for c in range(NCORES)]
        resB = run_B(layer, bins, inp)
        xT = [resB[c]["outT"] for c in range(NCORES)]
    out = np.zeros((2, SEQ, D), np.float32)
    for c in range(NCORES):
        b, sg = c // 4, c % 4
        out[b, sg * NT:(sg + 1) * NT, :] = xT[c].T
    return out
```
